# Optimizing a Trainium2 kernel written in Bass

```python
import math
import jax, jax.numpy as jnp
from jax import lax
import numpy as np

D_MODEL = 4096
BATCH = 1
SEQ = 8192
DEPTH = 1
DEC_BATCH = 32
DEC_SEQ = 8
PAST_LEN = 8192
PAGE_SIZE = 128

D_MIX = D_MODEL
ATTN_WIDTH = D_MIX // 2
POOL_WIDTH = D_MIX - ATTN_WIDTH
HEAD_DIM = 128
N_HEADS = ATTN_WIDTH // HEAD_DIM
N_KV_HEADS = max(1, N_HEADS // 4)
KV_GROUP = N_HEADS // N_KV_HEADS
KV_WIDTH = N_KV_HEADS * HEAD_DIM
ATTN_SCALE = HEAD_DIM ** -0.5
MOBA_BLOCK = 256
MOBA_TOPK = 3
Q_CHUNK = 16
POOL_WINDOWS = (2, 4, 8, 16)
N_POOL_GROUPS = len(POOL_WINDOWS)
POOL_GROUP_WIDTH = POOL_WIDTH // N_POOL_GROUPS
POOL_HIST = max(POOL_WINDOWS) - 1
NUM_BUCKETS = 32
MAX_EXACT = NUM_BUCKETS // 2
MAX_DISTANCE = 4096
RMS_EPS = 1e-6
NEG_INF = -1e30
IN_WIDTH = 2 * ATTN_WIDTH + 2 * KV_WIDTH + 2 * POOL_WIDTH
SPLITS = (ATTN_WIDTH, ATTN_WIDTH + KV_WIDTH, ATTN_WIDTH + 2 * KV_WIDTH,
          2 * ATTN_WIDTH + 2 * KV_WIDTH, 2 * ATTN_WIDTH + 2 * KV_WIDTH + POOL_WIDTH)

kernel_name = "moba_pool_hybrid_step"


def rms_norm(x, gain):
    xf = x.astype(jnp.float32)
    y = xf * lax.rsqrt(jnp.mean(xf * xf, axis=-1, keepdims=True) + RMS_EPS)
    return (y * gain.astype(jnp.float32)).astype(x.dtype)


def rel_bucket(dist):
    n = jnp.maximum(dist, 0)
    nf = jnp.maximum(n, 1).astype(jnp.float32)
    large = MAX_EXACT + (jnp.log(nf / MAX_EXACT) / math.log(MAX_DISTANCE / MAX_EXACT)
                         * (NUM_BUCKETS - MAX_EXACT)).astype(jnp.int32)
    large = jnp.minimum(large, NUM_BUCKETS - 1)
    return jnp.where(n < MAX_EXACT, n, large)


def ceil_blocks(length):
    return -(-length // MOBA_BLOCK) * MOBA_BLOCK


def pad_rows(x, length, axis):
    pad = [(0, 0)] * x.ndim
    pad[axis] = (0, length - x.shape[axis])
    return jnp.pad(x, pad)


def project(h, gain, w):
    n, t, _ = h.shape
    z = rms_norm(h, gain) @ w
    q, k, v, g_attn, u, g_pool = jnp.split(z, list(SPLITS), axis=-1)
    return (q.reshape(n, t, N_HEADS, HEAD_DIM),
            k.reshape(n, t, N_KV_HEADS, HEAD_DIM),
            v.reshape(n, t, N_KV_HEADS, HEAD_DIM),
            g_attn, u, g_pool)


def moba_chunk(qc, pc, bc, vc, k_full, v_full, rel_bias):
    kp = bc[..., None] * MOBA_BLOCK + jnp.arange(MOBA_BLOCK, dtype=jnp.int32)
    kvh = (jnp.arange(N_HEADS) // KV_GROUP)[None, :, None, None]
    kg = k_full[kp, kvh]
    vg = v_full[kp, kvh]
    logits = jnp.einsum('chd,chjkd->chjk', qc, kg).astype(jnp.float32) * ATTN_SCALE
    dist = pc[:, None, None, None] - kp
    hidx = jnp.arange(N_HEADS)[None, :, None, None]
    logits = logits + rel_bias[rel_bucket(dist), hidx].astype(jnp.float32)
    valid = vc[:, None, :, None] & (dist >= 0)
    logits = jnp.where(valid, logits, NEG_INF)
    c, h, j, b = logits.shape
    p = jax.nn.softmax(logits.reshape(c, h, j * b), axis=-1).reshape(c, h, j, b)
    return jnp.einsum('chjk,chjkd->chd', p.astype(vg.dtype), vg)


def moba_sequence(q, k_full, v_full, pos, rel_bias, q_chunk):
    n_q = q.shape[0]
    n_blk = k_full.shape[0] // MOBA_BLOCK
    kvh = jnp.arange(N_HEADS) // KV_GROUP
    k_mean = k_full.astype(jnp.float32).reshape(
        n_blk, MOBA_BLOCK, N_KV_HEADS, HEAD_DIM).mean(axis=1)[:, kvh]
    n_past = pos // MOBA_BLOCK
    gate = jnp.einsum('qhd,nhd->qhn', q.astype(jnp.float32), k_mean)
    gate = jnp.where(jnp.arange(n_blk)[None, None, :] < n_past[:, None, None], gate, NEG_INF)
    k_sel = min(MOBA_TOPK, n_blk)
    _, sel = lax.top_k(gate, k_sel)
    own = jnp.broadcast_to(n_past[:, None, None], (n_q, N_HEADS, 1)).astype(sel.dtype)
    blocks = jnp.concatenate([sel, own], axis=-1)
    n_j = k_sel + 1
    blk_valid = jnp.concatenate(
        [jnp.arange(k_sel)[None, :] < n_past[:, None], jnp.ones((n_q, 1), dtype=bool)], axis=-1)
    n_c = n_q // q_chunk
    xs = (q.reshape(n_c, q_chunk, N_HEADS, HEAD_DIM), pos.reshape(n_c, q_chunk),
          blocks.reshape(n_c, q_chunk, N_HEADS, n_j), blk_valid.reshape(n_c, q_chunk, n_j))
    out = lax.map(lambda a: moba_chunk(a[0], a[1], a[2], a[3], k_full, v_full, rel_bias), xs)
    return out.reshape(n_q, N_HEADS, HEAD_DIM)


def pool_mixer(u_ext, pos, w_pool, pool_scale):
    n, total, _ = u_ext.shape
    t = total - POOL_HIST
    cs = jnp.concatenate([jnp.zeros((n, 1, POOL_WIDTH), jnp.float32),
                          jnp.cumsum(u_ext, axis=1)], axis=1)
    u_new = u_ext[:, POOL_HIST:]
    parts = []
    for g, w in enumerate(POOL_WINDOWS):
        sl = slice(g * POOL_GROUP_WIDTH, (g + 1) * POOL_GROUP_WIDTH)
        win_sum = (cs[:, POOL_HIST + 1:POOL_HIST + 1 + t, sl]
                   - cs[:, POOL_HIST + 1 - w:POOL_HIST + 1 - w + t, sl])
        count = jnp.minimum(w, pos + 1).astype(jnp.float32)
        parts.append(win_sum / count[None, :, None] - u_new[..., sl])
    z = jnp.stack(parts, axis=2)
    y = jnp.einsum('ntgc,gcd->ntgd', z, w_pool.astype(jnp.float32))
    return y.reshape(n, t, POOL_WIDTH) * pool_scale.astype(jnp.float32)


def merge(h, attn, g_attn, pooled, g_pool, w):
    n, t, _ = h.shape
    y_a = attn.reshape(n, t, ATTN_WIDTH) * jax.nn.silu(g_attn)
    y_p = pooled.astype(h.dtype) * jax.nn.silu(g_pool)
    return h + jnp.concatenate([y_a, y_p], axis=-1) @ w


def setup_inputs(seed: int = 0) -> dict:
    key = jax.random.key(seed)
    ks = jax.random.split(key, 13)
    n_pages = PAST_LEN // PAGE_SIZE
    n_used = DEC_BATCH * n_pages
    n_phys = n_used + (n_used + 3) // 4
    nrm = lambda k, shape: jax.random.normal(k, shape, jnp.float32)
    return {
        "x_prompt": nrm(ks[0], (BATCH, SEQ, D_MODEL)),
        "x_sample": nrm(ks[1], (DEC_BATCH, DEC_SEQ, D_MODEL)),
        "cache_k": nrm(ks[2], (DEPTH, n_phys, PAGE_SIZE, N_KV_HEADS, HEAD_DIM)),
        "cache_v": nrm(ks[3], (DEPTH, n_phys, PAGE_SIZE, N_KV_HEADS, HEAD_DIM)),
        "state_pool": nrm(ks[4], (DEPTH, DEC_BATCH, POOL_HIST, POOL_WIDTH)),
        "page_table": jax.random.permutation(ks[5], n_phys)[:n_used].reshape(
            DEC_BATCH, n_pages).astype(jnp.int32),
        "norm_in": 1.0 + 0.1 * nrm(ks[6], (DEPTH, D_MODEL)),
        "w_in": nrm(ks[7], (DEPTH, D_MODEL, IN_WIDTH)) * D_MODEL ** -0.5,
        "w_pool": nrm(ks[8], (DEPTH, N_POOL_GROUPS, POOL_GROUP_WIDTH, POOL_GROUP_WIDTH))
                  * POOL_GROUP_WIDTH ** -0.5,
        "pool_scale": 1.0 + 0.1 * nrm(ks[9], (DEPTH, POOL_WIDTH)),
        "w_out": nrm(ks[10], (DEPTH, D_MIX, D_MODEL)) * D_MIX ** -0.5,
        "rel_bias": 0.5 * nrm(ks[11], (NUM_BUCKETS, N_HEADS)),
        "norm_out": 1.0 + 0.1 * nrm(ks[12], (D_MODEL,)),
    }


def reference(x_prompt, x_sample, cache_k, cache_v, state_pool, page_table,
              norm_in, w_in, w_pool, pool_scale, w_out, rel_bias, norm_out):
    pos_p = jnp.arange(SEQ, dtype=jnp.int32)
    pos_s = PAST_LEN + jnp.arange(DEC_SEQ, dtype=jnp.int32)
    len_p = ceil_blocks(SEQ)
    len_s = ceil_blocks(PAST_LEN + DEC_SEQ)
    h_p, h_s = x_prompt, x_sample
    kp_l, vp_l, pp_l, ks_l, vs_l, ps_l = [], [], [], [], [], []
    for layer in range(DEPTH):
        q, k, v, g_a, u, g_p = project(h_p, norm_in[layer], w_in[layer])
        k_pad = pad_rows(k, len_p, 1)
        v_pad = pad_rows(v, len_p, 1)
        attn = lax.map(lambda a: moba_sequence(a[0], a[1], a[2], pos_p, rel_bias, Q_CHUNK),
                       (q, k_pad, v_pad))
        u_ext = jnp.concatenate([jnp.zeros((BATCH, POOL_HIST, POOL_WIDTH), jnp.float32),
                                 u.astype(jnp.float32)], axis=1)
        pooled = pool_mixer(u_ext, pos_p, w_pool[layer], pool_scale[layer])
        h_p = merge(h_p, attn, g_a, pooled, g_p, w_out[layer])
        kp_l.append(k)
        vp_l.append(v)
        pp_l.append(u_ext[:, -POOL_HIST:].astype(x_prompt.dtype))
        q, k, v, g_a, u, g_p = project(h_s, norm_in[layer], w_in[layer])
        ck = cache_k[layer]
        cv = cache_v[layer]

        def attend_one(a, ck=ck, cv=cv):
            qb, kb, vb, pt = a
            k_full = jnp.concatenate([ck[pt].reshape(-1, N_KV_HEADS, HEAD_DIM), kb], axis=0)
            v_full = jnp.concatenate([cv[pt].reshape(-1, N_KV_HEADS, HEAD_DIM), vb], axis=0)
            return moba_sequence(qb, pad_rows(k_full, len_s, 0), pad_rows(v_full, len_s, 0),
                                 pos_s, rel_bias, DEC_SEQ)

        attn = lax.map(attend_one, (q, k, v, page_table))
        u_ext = jnp.concatenate([state_pool[layer].astype(jnp.float32),
                                 u.astype(jnp.float32)], axis=1)
        pooled = pool_mixer(u_ext, pos_s, w_pool[layer], pool_scale[layer])
        h_s = merge(h_s, attn, g_a, pooled, g_p, w_out[layer])
        ks_l.append(k)
        vs_l.append(v)
        ps_l.append(u_ext[:, -POOL_HIST:].astype(state_pool.dtype))
    y_prompt = rms_norm(h_p, norm_out)
    y_sample = rms_norm(h_s, norm_out)
    k_prompt = jnp.stack(kp_l)
    v_prompt = jnp.stack(vp_l)
    pool_prompt = jnp.stack(pp_l)
    k_sample = jnp.stack(ks_l)
    v_sample = jnp.stack(vs_l)
    pool_sample = jnp.stack(ps_l)
    return (y_prompt, y_sample, k_prompt, v_prompt, pool_prompt, k_sample, v_sample, pool_sample)
```

```python
import math
import numpy as np
import ml_dtypes
import jax
import jax.numpy as jnp
import concourse.bass as bass
import concourse.mybir as mybir
from concourse.bass_utils import run_bass_kernel_spmd

F32 = mybir.dt.float32
BF16 = mybir.dt.bfloat16
I32 = mybir.dt.int32
AF = mybir.ActivationFunctionType
ALU = mybir.AluOpType
AX = mybir.AxisListType

NCORE = 8
D = 4096
SEQ = 8192
NT = 8
NTOK = 1184
NQ = 1056
INW = 9216
SW = 256
SCALE = 128 ** -0.5
EPS = 1e-6
NEG = -30000.0
TL = 4096
GEXT = TL + 128

DO_PATT = True
DO_SATT = True
DO_BOUND = True


class Res:
    __slots__ = ("name", "w", "r", "rd")

    def __init__(self, name=""):
        self.name = name
        self.w = None
        self.r = {}
        self.rd = []


class Op:
    __slots__ = ("eng", "fn", "deps", "needed", "sem", "val", "waits", "ndma", "is_dma", "inc")

    def __init__(self, eng, fn, sem=None, ndma=1, inc=16):
        self.inc = inc
        self.eng = eng
        self.fn = fn
        self.deps = set()
        self.needed = False
        self.sem = sem
        self.val = None
        self.waits = []
        self.ndma = ndma
        self.is_dma = sem is not None


ENGS = ["pe", "act", "dve", "pool", "sp"]


class Sched:
    def __init__(self):
        self.ops = {e: [] for e in ENGS}
        self.all = []
        self.last = {e: None for e in ENGS}
        self.dmas = []

    def add(self, eng, fn, reads=(), writes=(), sem=None, ndma=1, inc=16):
        op = Op(eng, fn, sem, ndma, inc)
        deps = op.deps
        for r in reads:
            if r.w is not None:
                deps.add(r.w)
        for r in writes:
            if r.w is not None:
                deps.add(r.w)
            deps.update(r.r.values())
            deps.update(r.rd)
        for r in reads:
            if op.is_dma:
                r.rd.append(op)
            else:
                r.r[eng] = op
        for r in writes:
            r.w = op
            r.r = {}
            r.rd = []
        self.ops[eng].append(op)
        self.all.append(op)
        if fn is not None and not op.is_dma:
            self.last[eng] = op
        if op.is_dma:
            self.dmas.append(op)
        return op

    def barrier(self):
        deps = set(o for o in self.last.values() if o is not None)
        deps.update(self.dmas)
        self.dmas = []
        for e in ENGS:
            op = Op(e, None)
            op.deps = set(deps)
            self.ops[e].append(op)
            self.all.append(op)

    def finalize(self, esem):
        for op in self.all:
            for d in op.deps:
                d.needed = True
        cnt = {e: 0 for e in ENGS}
        dcnt = {}
        for op in self.all:
            if op.is_dma:
                k = id(op.sem)
                dcnt[k] = dcnt.get(k, 0) + op.inc * op.ndma
                op.val = dcnt[k]
            elif op.fn is not None and op.needed:
                cnt[op.eng] += 1
                op.val = cnt[op.eng]
                op.sem = esem[op.eng]
        for e in ENGS:
            waited = {}
            for op in self.ops[e]:
                need = {}
                for d in op.deps:
                    k = id(d.sem)
                    if k not in need or need[k][1] < d.val:
                        need[k] = (d.sem, d.val)
                for k, (s, v) in need.items():
                    if waited.get(k, 0) >= v:
                        continue
                    waited[k] = v
                    op.waits.append((s, v))

    def emit(self, eng, e):
        for op in self.ops[eng]:
            for (s, v) in op.waits:
                e.wait_ge(s, v)
            if op.fn is None:
                continue
            ins = op.fn(e)
            if op.is_dma:
                pass
            elif op.needed:
                ins.then_inc(op.sem, 1)


def build_nc():
    nc = bass.Bass("TRN2", target_bir_lowering=False)

    def din(name, shape, dt=F32):
        return nc.dram_tensor(name, list(shape), dt, kind="ExternalInput")

    def dout(name, shape, dt=F32):
        return nc.dram_tensor(name, list(shape), dt, kind="ExternalOutput")

    xall = din("xall", [NTOK, D])
    w_in = din("w_in", [D, INW])
    w_out = din("w_out", [D, D])
    w_pool = din("w_pool", [4 * 512, 512])
    gain_in = din("gain_in", [1, D])
    gain_out = din("gain_out", [1, D])
    pscale = din("pscale", [128, 16])
    cache_k = din("cache_k", [2560 * 128, 512])
    cache_v = din("cache_v", [2560 * 128, 512])
    ptab = din("ptab", [1, 256], I32)
    stpool = din("stpool", [4 * 15, 2048])
    gext = din("gext", [16, GEXT])
    rbs = din("rbs", [4 * 128, 2048])
    rbn = din("rbn", [32, 512])
    bands = din("bands", [128, 40 * 128], BF16)
    bands_s = din("bands_s", [64, 256], BF16)
    bmask = din("bmask", [128, 8 * 32])
    ident_d = din("ident", [128, 128], BF16)
    identf_d = din("identf", [128, 128])
    sel_d = din("sel", [128, 33 * 128], BF16)
    iota_d = din("iota", [128, 1])
    jrev_d = din("jrev", [128, 128], BF16)

    y_o = dout("y", [NQ, D])
    kp_o = dout("kp", [1024, 512])
    vp_o = dout("vp", [1024, 512])
    pp_o = dout("pp", [15, 2048])
    ks_o = dout("ks", [32, 512])
    vs_o = dout("vs", [32, 512])
    ps_o = dout("ps", [4 * 15, 2048])

    kin = nc.dram_tensor("kin", [512, 1024], BF16)
    kall = nc.dram_tensor("kall", [4096, 1024], BF16)
    vin = nc.dram_tensor("vin", [512, 1040], BF16)
    vall = nc.dram_tensor("vall", [4096, 1040], BF16)
    mixp = nc.dram_tensor("mixp", [16 * 128, NQ], BF16)

    S = Sched()
    import contextlib
    es = contextlib.ExitStack()

    def sb(name, shape, dt):
        return es.enter_context(nc.sbuf_tensor(name, list(shape), dt))

    nsem = [0]

    def newsem(name):
        nsem[0] += 1
        return es.enter_context(nc.semaphore(f"{name}_{nsem[0]}"))

    esem = {e: newsem("e_" + e) for e in ENGS}

    RX = sb("RX", [128, 32 * NTOK], BF16)
    RA = sb("RA", [128, 33280], BF16)
    RS = sb("RS", [128, 16384], BF16)

    def carve(reg, off_b, nbytes, dt):
        a = reg[:, off_b // 2:(off_b + nbytes) // 2]
        if dt is not BF16:
            a = a.bitcast(dt)
        return a

    ident = sb("identb", [128, 128], BF16)
    identf = sb("identfs", [128, 128], F32)
    sel = sb("selb", [128, 33 * 128], BF16)
    jrev = sb("jrevb", [128, 128], BF16)
    iota_p = sb("iotap", [128, 1], F32)
    ones_bf = sb("onesbf", [128, 128], BF16)
    eps_t = sb("epst", [128, 1], F32)
    ssq = sb("ssq", [128, 16], F32)
    rstd = sb("rstd", [128, 16], F32)
    bmask_t = sb("bmaskt", [128, 8 * 32], F32)
    pscale_t = sb("pscalet", [128, 16], F32)
    kst = [sb(f"kst{i}", [128, SW], F32) for i in range(2)]
    kbt = [sb(f"kbt{i}", [128, SW], BF16) for i in range(2)]
    ktst = [sb(f"ktst{i}", [128, 2 * 128], BF16) for i in range(2)]
    vbt = [sb(f"vbt{i}", [128, 2 * 130], BF16) for i in range(2)]
    ktnew = sb("ktnew", [128, 4 * 32], BF16)
    vbnew = sb("vbnew", [32, 4 * 130], BF16)
    gats = sb("gats", [128, 16 * 32], BF16)
    ssq2 = sb("ssq2", [128, 9 * 16], F32)
    qn = sb("qn", [128, 9 * 16], F32)
    nbq = sb("nbq", [128, 9 * 16], F32)
    gm = sb("gm", [128, 32], F32)
    mx8 = sb("mx8", [128, 8], F32)
    selt = sb("selt", [128, 32], F32)
    mrow = sb("mrow", [128, 64], BF16)
    mrT = [sb(f"mrT{i}", [128, 128], BF16) for i in range(2)]
    pbuf = [sb(f"pbuf{i}", [128, 512], BF16) for i in range(4)]
    rsum = sb("rsum", [128, 1], F32)
    yat = sb("yat", [128, 128], BF16)
    kmT = sb("kmT", [128, 32], F32)
    kmb = sb("kmb", [128, 32], BF16)
    ks64 = sb("ks64", [128, 64], F32)
    sqc = sb("sqc", [128, 512], BF16)
    kmx = sb("kmx", [1, 80], F32)
    kmx1 = sb("kmx1", [1, 2], F32)
    nkmax = sb("nkmax", [128, 1], F32)
    idxf = sb("idxf", [128, 256], F32)
    idxi = sb("idxi", [128, 256], I32)
    ptab_t = sb("ptabt", [128, 256], I32)
    qsq = sb("qsq", [128, NQ], BF16)
    dgs = sb("dgs", [32, 8], F32)
    junk2 = sb("junk2", [128, SW], BF16)
    pnew = sb("pnew", [32, 128], BF16)
    qs = sb("qs", [128, 128], BF16)
    r_qs = Res("qs")
    r_pnew = Res("pnew")

    PS = [es.enter_context(nc.psum_tensor(f"ps{i}", [128, 512], F32)) for i in range(8)]
    PSR = [Res(f"ps{i}") for i in range(8)]

    def psb(i):
        return PS[i][:, :].bitcast(BF16)

    XT = RX[:, :].rearrange("p (k t) -> p k t", k=32)
    XTR = Res("xt")

    xt_t = [carve(RA, 0, 16384, F32), carve(RA, 16384, 16384, F32)]
    xnb = carve(RA, 32768, 8192, BF16)
    gain_t = carve(RA, 40960, 16384, F32)
    junk = carve(RA, 57344, 8192, BF16)
    Ug = carve(RA, 0, 10240, BF16).rearrange("p (t c) -> p t c", t=10)
    ZT = carve(RA, 10240, 8448, BF16).rearrange("p (c t) -> p c t", c=4)
    GPT = carve(RA, 18688, 8448, BF16).rearrange("p (c t) -> p c t", c=4)
    WP = [carve(RA, 27136 + i * 4096, 4096, BF16).rearrange("p (c d) -> p c d", c=4) for i in range(2)]
    MPst = [carve(RA, 35328 + i * 2112, 2112, BF16) for i in range(2)]
    BND = carve(RA, 39552, 10240, BF16)
    STb = carve(RA, 49792, 4096, BF16)
    BNDS = carve(RA, 53888, 512, BF16)
    ust = [carve(RA, 54400 + i * 1024, 1024, F32) for i in range(2)]
    QT = carve(RA, 0, 16 * NQ * 2, BF16).rearrange("p (h t) -> p h t", h=16)
    GA = carve(RA, 16 * NQ * 2, 32768, BF16).rearrange("p (t c) -> p t c", t=8)
    slab = [carve(RS, i * 16384, 16384, BF16).rearrange("p (k n) -> p k n", k=32) for i in range(2)]
    TH = [carve(RS, i * 8192, 8192, BF16) for i in range(2)]
    THst = carve(RS, 16384, 16384, F32)
    KT = [carve(RX, i * 16384, 16384, BF16) for i in range(2)]
    VV = [carve(RX, 32768 + i * 16640, 16640, BF16).rearrange("p (s c) -> p s c", s=64) for i in range(2)]
    KTs = carve(RX, 0, 65536, BF16).rearrange("p (h n) -> p h n", h=4)
    MP = carve(RX, 0, 16 * NQ * 2, BF16).rearrange("p (h t) -> p h t", h=16)
    ybuf = [carve(RX, 33792 + i * 16384, 16384, F32) for i in range(2)]
    xsl = carve(RX, 33792 + 32768, 9 * SW * 4, F32).rearrange("p (t c) -> p t c", t=9)
    KST = [carve(RS, i * 8192, 8192, F32).rearrange("p (g c) -> p g c", g=4) for i in range(2)]
    Esst = carve(RS, 16384, 8192, F32)
    Es = carve(RS, 24576, 4096, BF16)
    KB = carve(RA, 16 * NQ * 2, 4096, BF16).rearrange("p (g c) -> p g c", g=4)
    VBc = carve(RA, 16 * NQ * 2 + 4096, 4 * 4 * 130 * 2, BF16).rearrange("p (g h c) -> p g h c", g=4, h=4)
    Ps_all = carve(RA, 16 * NQ * 2 + 8256, 16384, BF16).rearrange("p (h g q) -> p h g q", h=4, g=64)
    gout_t = carve(RA, 16 * NQ * 2, 16384, F32)

    def dma(eng, out, in_, sem, reads=(), writes=(), **kw):
        def fn(e):
            return e.dma_start(out=out, in_=in_, **kw).then_inc(sem, 16)
        return S.add(eng, fn, reads=reads, writes=writes, sem=sem)

    class R:
        pass
    r = R()
    r.consts = Res("consts")
    r.xt = [Res("xt0"), Res("xt1")]
    r.xnb = Res("xnb")
    r.gain = Res("gain")
    r.junk = Res("junk")
    r.ssq = Res("ssq")
    r.rstd = Res("rstd")
    r.XT = [Res(f"XT{i}") for i in range(10)]
    r.slab = [Res("slab0"), Res("slab1")]
    r.kst = [Res("kst0"), Res("kst1")]
    r.kbt = [Res("kbt0"), Res("kbt1")]
    r.ktst = [Res("ktst0"), Res("ktst1")]
    r.vbt = [Res("vbt0"), Res("vbt1")]
    r.ktnew = Res("ktnew")
    r.vbnew = Res("vbnew")
    r.kin = Res("kin")
    r.vin = Res("vin")
    r.kall = Res("kall")
    r.vall = Res("vall")
    r.outs = Res("outs")
    r.Ug = Res("Ug")
    r.ZT = Res("ZT")
    r.GPT = Res("GPT")
    r.WP = [Res("WP0"), Res("WP1")]
    r.MPst = [Res("MPst0"), Res("MPst1")]
    r.ust = [Res("ust0"), Res("ust1")]
    r.STb = Res("STb")
    r.mixp = Res("mixp")
    r.QT = [[Res(f"QT{h}_{t}") for t in range(9)] for h in range(16)]
    r.GA = [Res(f"GA{t}") for t in range(8)]
    r.gats = Res("gats")

    sem_c = newsem("c")
    sem_x = [newsem("x0"), newsem("x1")]
    sem_slab = [newsem("sl0"), newsem("sl1")]
    sem_kst = [newsem("kst0"), newsem("kst1")]
    sem_ktst = [newsem("ktst0"), newsem("ktst1")]
    sem_vbt = [newsem("vbt0"), newsem("vbt1")]
    sem_ust = [newsem("ust0"), newsem("ust1")]
    sem_wp = [newsem("wp0"), newsem("wp1")]
    sem_mp = [newsem("mp0"), newsem("mp1")]
    sem_misc = newsem("misc")
    sem_cc = newsem("cc")

    out_final = []

    def load_const(dst, src, res):
        dma("sp", dst, src, newsem("lc"), writes=[res])

    cres = [Res(f"c{i}") for i in range(16)]
    load_const(ident[:, :], ident_d.ap(), cres[0])
    load_const(jrev[:, :], jrev_d.ap(), cres[11])
    load_const(identf[:, :], identf_d.ap(), cres[1])
    load_const(sel[:, :], sel_d.ap(), cres[2])
    load_const(iota_p[:, :], iota_d.ap(), cres[3])
    load_const(bmask_t[:, :], bmask.ap(), cres[4])
    load_const(pscale_t[:, :], pscale.ap(), cres[5])
    load_const(gain_t, bass.AP(gain_in, 0, [[0, 128], [1, D]]), r.gain)
    S.add("dve", lambda e: e.memset(ones_bf[:, :], 1.0), writes=[cres[6]])
    S.add("dve", lambda e: e.memset(eps_t[:, :], EPS), writes=[cres[7]])
    S.add("dve", lambda e: e.memset(ssq[:, :], 0.0), writes=[r.ssq])
    S.add("dve", lambda e: e.memset(ssq2[:, :], 0.0), writes=[cres[8]])
    for i in range(2):
        S.add("dve", (lambda i: lambda e: e.memset(vbt[i][:, :], 0.0))(i), writes=[r.vbt[i]])
        S.add("dve", (lambda i: lambda e: e.memset(
            vbt[i][:, :].rearrange("p (h c) -> p h c", h=2)[:, :, 128:129], 1.0))(i), writes=[r.vbt[i]])
    S.add("dve", lambda e: e.memset(vbnew[:, :], 0.0), writes=[r.vbnew])
    S.add("dve", lambda e: e.memset(
        vbnew[:, :].rearrange("p (h c) -> p h c", h=4)[:, :, 128:129], 1.0), writes=[r.vbnew])

    def tile_rows(i):
        return 32 if i == 9 else 128

    for i in range(10):
        b = i % 2
        rows = tile_rows(i)
        r0 = i * 128
        dma("sp", xt_t[b][0:rows, :], xall[r0:r0 + rows, :], sem_x[b], writes=[r.xt[b]])
        S.add("act", (lambda b, rows, i: lambda e: e.activation(
            out=junk[0:rows, :], in_=xt_t[b][0:rows, :], func=AF.Square,
            accum_out=ssq[0:rows, i:i + 1]))(b, rows, i),
            reads=[r.xt[b]], writes=[r.junk, r.ssq])
        S.add("act", (lambda rows, i: lambda e: e.activation(
            out=rstd[0:rows, i:i + 1], in_=ssq[0:rows, i:i + 1], func=AF.Sqrt,
            bias=eps_t[0:rows, :], scale=1.0 / D))(rows, i),
            reads=[r.ssq, cres[7]], writes=[r.rstd])
        S.add("dve", (lambda rows, i: lambda e: e.reciprocal(
            out=rstd[0:rows, i:i + 1], in_=rstd[0:rows, i:i + 1]))(rows, i),
            reads=[r.rstd], writes=[r.rstd])
        S.add("dve", (lambda b, rows, i: lambda e: e.scalar_tensor_tensor(
            out=xnb[0:rows, :], in0=xt_t[b][0:rows, :], scalar=rstd[0:rows, i:i + 1],
            in1=gain_t[0:rows, :], op0=ALU.mult, op1=ALU.mult))(b, rows, i),
            reads=[r.xt[b], r.rstd, r.gain], writes=[r.xnb])
        for q4 in range(4):
            bank = q4 % 4

            def tfn(e, rows=rows, q4=q4, bank=bank):
                ins = None
                pv = psb(bank).rearrange("p (k t) -> p k t", k=8)
                for k8 in range(8):
                    kc = q4 * 8 + k8
                    ins = e.transpose(out=pv[:, k8, 0:rows], in_=xnb[0:rows, kc * 128:(kc + 1) * 128],
                                      identity=ident[0:rows, 0:rows])
                return ins
            S.add("pe", tfn, reads=[r.xnb, cres[0]], writes=[PSR[bank]])
            ceng = "act" if q4 % 2 == 0 else "dve"

            def cfn(e, rows=rows, q4=q4, bank=bank, i=i, ceng=ceng, r0=r0):
                pv = psb(bank).rearrange("p (k t) -> p k t", k=8)
                o = XT[:, q4 * 8:(q4 + 1) * 8, r0:r0 + rows]
                if ceng == "act":
                    return e.copy(out=o, in_=pv[:, :, 0:rows])
                return e.tensor_copy(out=o, in_=pv[:, :, 0:rows])
            S.add(ceng, cfn, reads=[PSR[bank]], writes=[r.XT[i]])
    S.barrier()

    slab_ctr = [0]
    w_in_v = w_in.ap().rearrange("(k p) n -> p k n", p=128)
    w_out_v = w_out.ap().rearrange("(k p) n -> p k n", p=128)

    def load_slab(wv, c0):
        b = slab_ctr[0] % 2
        slab_ctr[0] += 1
        def fn(e, b=b, c0=c0):
            ins = None
            for k4 in range(4):
                ins = e.dma_start(out=slab[b][:, k4 * 8:(k4 + 1) * 8, :],
                                  in_=wv[:, k4 * 8:(k4 + 1) * 8, c0:c0 + SW]).then_inc(sem_slab[b], 16)
            return ins
        S.add("pool", fn, writes=[r.slab[b]], sem=sem_slab[b], ndma=4)
        return b

    psctr = [0]

    def next_bank(lo=0, hi=8):
        b = lo + psctr[0] % (hi - lo)
        psctr[0] += 1
        return b

    def mm_tok(b, i, bank):
        rows = tile_rows(i)
        t0 = i * 128

        def fn(e):
            ins = None
            for kc in range(32):
                ins = e.matmul(out=PS[bank][0:rows, 0:SW], lhsT=XT[:, kc, t0:t0 + rows],
                               rhs=slab[b][:, kc, :], start=(kc == 0), stop=(kc == 31))
            return ins
        S.add("pe", fn, reads=[r.XT[i], r.slab[b]], writes=[PSR[bank]])

    def mm_feat(b, m, t0, n, bank):
        def fn(e):
            ins = None
            for kc in range(32):
                ins = e.matmul(out=PS[bank][:, 0:n], lhsT=slab[b][:, kc, m * 128:(m + 1) * 128],
                               rhs=XT[:, kc, t0:t0 + n], start=(kc == 0), stop=(kc == 31))
            return ins
        tiles = [r.XT[t0 // 128 + k] for k in range(max(1, n // 128))] if t0 < 1152 else [r.XT[9]]
        S.add("pe", fn, reads=tiles + [r.slab[b]], writes=[PSR[bank]])

    out_sem = newsem("outst")

    def store_out(eng, out, in_, sem, reads):
        op = dma(eng, out, in_, sem, reads=reads)
        out_final.append(op)
        return op

    kvctr = [0]
    for si in range(2):
        b = load_slab(w_in_v, 2048 + si * SW)
        for i in list(range(8)) + [9]:
            rows = tile_rows(i)
            bank = next_bank(0, 4)
            mm_tok(b, i, bank)
            sbi = kvctr[0] % 2
            kvctr[0] += 1
            S.add("act", (lambda rows, bank, sbi: lambda e: e.copy(
                out=kst[sbi][0:rows, :], in_=PS[bank][0:rows, 0:SW]))(rows, bank, sbi),
                reads=[PSR[bank]], writes=[r.kst[sbi]])
            S.add("dve", (lambda rows, sbi: lambda e: e.tensor_copy(
                out=kbt[sbi][0:rows, :], in_=kst[sbi][0:rows, :]))(rows, sbi),
                reads=[r.kst[sbi]], writes=[r.kbt[sbi]])
            if i < 8:
                store_out("sp", kp_o[i * 128:(i + 1) * 128, si * SW:(si + 1) * SW], kst[sbi][:, :],
                          sem_kst[sbi], [r.kst[sbi]])
            else:
                store_out("sp", ks_o[0:32, si * SW:(si + 1) * SW], kst[sbi][0:32, :],
                          sem_kst[sbi], [r.kst[sbi]])
            tb = 4 + sbi

            def ktr(e, rows=rows, sbi=sbi, tb=tb):
                ins = None
                for hh in range(2):
                    rh = jrev[:, :] if rows == 128 else ident[0:rows, 0:rows]
                    ins = e.matmul(out=PS[tb][:, hh * 128:hh * 128 + rows],
                                   lhsT=kbt[sbi][0:rows, hh * 128:(hh + 1) * 128], rhs=rh,
                                   start=True, stop=True)
                return ins
            S.add("pe", ktr, reads=[r.kbt[sbi], cres[0], cres[11]], writes=[PSR[tb]])
            if i < 8:
                S.add("act", (lambda sbi, tb: lambda e: e.copy(
                    out=ktst[sbi][:, :], in_=PS[tb][:, 0:256]))(sbi, tb),
                    reads=[PSR[tb]], writes=[r.ktst[sbi]])
                kin_v = kin.ap().rearrange("(h d) n -> d h n", h=4)
                dma("sp", kin_v[:, 2 * si:2 * si + 2, i * 128:(i + 1) * 128],
                    ktst[sbi][:, :].rearrange("p (h n) -> p h n", h=2), sem_ktst[sbi],
                    reads=[r.ktst[sbi]], writes=[r.kin])
            else:
                S.add("act", (lambda si, tb: lambda e: e.copy(
                    out=ktnew[:, :].rearrange("p (h n) -> p h n", h=4)[:, 2 * si:2 * si + 2, :],
                    in_=PS[tb][:, 0:256].rearrange("p (h n) -> p h n", h=2)[:, :, 0:32]))(si, tb),
                    reads=[PSR[tb]], writes=[r.ktnew])
    for si in range(2):
        b = load_slab(w_in_v, 2560 + si * SW)
        for i in list(range(8)) + [9]:
            rows = tile_rows(i)
            bank = next_bank(0, 4)
            mm_tok(b, i, bank)
            sbi = kvctr[0] % 2
            kvctr[0] += 1
            S.add("act", (lambda rows, bank, sbi: lambda e: e.copy(
                out=kst[sbi][0:rows, :], in_=PS[bank][0:rows, 0:SW]))(rows, bank, sbi),
                reads=[PSR[bank]], writes=[r.kst[sbi]])
            if i < 8:
                store_out("sp", vp_o[i * 128:(i + 1) * 128, si * SW:(si + 1) * SW], kst[sbi][:, :],
                          sem_kst[sbi], [r.kst[sbi]])
                S.add("dve", (lambda sbi: lambda e: e.tensor_copy(
                    out=kbt[sbi][:, :], in_=kst[sbi][:, :]))(sbi),
                    reads=[r.kst[sbi]], writes=[r.kbt[sbi]])
                tb = 4 + sbi
                S.add("pe", (lambda sbi, tb: lambda e: e.matmul(
                    out=PS[tb][:, 0:SW], lhsT=jrev[:, :], rhs=kbt[sbi][:, :], start=True, stop=True))(sbi, tb),
                    reads=[r.kbt[sbi], cres[11]], writes=[PSR[tb]])
                S.add("dve", (lambda sbi, tb: lambda e: e.tensor_copy(
                    out=vbt[sbi][:, :].rearrange("p (h c) -> p h c", h=2)[:, :, 0:128],
                    in_=PS[tb][:, 0:SW].rearrange("p (h c) -> p h c", h=2)))(sbi, tb),
                    reads=[PSR[tb]], writes=[r.vbt[sbi]])
                vin_v = vin.ap().rearrange("(h r) (j c) -> r h j c", h=4, c=130)
                dma("sp", vin_v[:, 2 * si:2 * si + 2, i, :],
                    vbt[sbi][:, :].rearrange("p (h c) -> p h c", h=2), sem_vbt[sbi],
                    reads=[r.vbt[sbi]], writes=[r.vin])
            else:
                store_out("sp", vs_o[0:32, si * SW:(si + 1) * SW], kst[sbi][0:32, :],
                          sem_kst[sbi], [r.kst[sbi]])
                S.add("dve", (lambda si, sbi: lambda e: e.tensor_copy(
                    out=vbnew[:, :].rearrange("p (h c) -> p h c", h=4)[:, 2 * si:2 * si + 2, 0:128],
                    in_=kst[sbi][0:32, :].rearrange("p (h c) -> p h c", h=2)))(si, sbi),
                    reads=[r.kst[sbi]], writes=[r.vbnew])

    sem_cc2 = newsem("cc2")

    def ccfn(kind_in, kind_out, sem):
        def fn(e):
            return e.collective_compute("AllGather", ALU.bypass, replica_groups=[list(range(NCORE))],
                                        ins=[kind_in.ap().opt()], outs=[kind_out.ap().opt()]).then_inc(sem, 1)
        return fn
    S.add("pool", ccfn(kin, kall, sem_cc), reads=[r.kin], writes=[r.kall], sem=sem_cc, inc=1)
    S.add("pool", ccfn(vin, vall, sem_cc2), reads=[r.vin], writes=[r.vall], sem=sem_cc2, inc=1)

    load_const(BND, bands.ap(), cres[9])
    load_const(BNDS[0:64, :], bands_s.ap(), cres[10])
    S.add("dve", lambda e: e.memset(STb[0:64, :], 0.0), writes=[r.STb])
    sem_stb = newsem("stb")

    def stbfn(e):
        ins = None
        for s4 in range(4):
            ins = e.dma_start(out=STb[16 * s4:16 * s4 + 15, :],
                              in_=stpool[15 * s4:15 * s4 + 15, :]).then_inc(sem_stb, 16)
        return ins
    S.add("pool", stbfn, writes=[r.STb], sem=sem_stb, ndma=4)
    st_v = stpool.ap().rearrange("(s i) c -> s i c", i=15)
    ps_v = ps_o.ap().rearrange("(s i) c -> s i c", i=15)
    store_out("sp", ps_v[:, 0:7, :], st_v[:, 8:15, :], sem_misc, [])

    uctr = [0]
    for g in range(4):
        bw = g % 2
        dma("pool", WP[bw], w_pool.ap().rearrange("(g k p) d -> g p k d", g=4, p=128)[g], sem_wp[bw],
            writes=[r.WP[bw]])
        for si in range(2):
            b = load_slab(w_in_v, 5120 + g * 512 + si * SW)
            for i in range(10):
                rows = tile_rows(i)
                bank = next_bank(0, 4)
                mm_tok(b, i, bank)
                S.add("act", (lambda rows, bank, i, si: lambda e: e.copy(
                    out=Ug[0:rows, i, si * SW:(si + 1) * SW], in_=PS[bank][0:rows, 0:SW]))(rows, bank, i, si),
                    reads=[PSR[bank]], writes=[r.Ug] + ([PSR[bank]] if i in (7, 9) else []))
                if i == 7 or i == 9:
                    ub = uctr[0] % 2
                    uctr[0] += 1
                    S.add("dve", (lambda rows, bank, ub: lambda e: e.tensor_copy(
                        out=ust[ub][0:rows, :], in_=PS[bank][0:rows, 0:SW]))(rows, bank, ub),
                        reads=[PSR[bank]], writes=[r.ust[ub]])
                    c0 = g * 512 + si * SW
                    if i == 7:
                        store_out("sp", pp_o[0:15, c0:c0 + SW], ust[ub][113:128, :], sem_ust[ub], [r.ust[ub]])
                    else:
                        def pfn(e, ub=ub, c0=c0):
                            ins = None
                            for s4 in range(4):
                                ins = e.dma_start(out=ps_o[15 * s4 + 7:15 * s4 + 15, c0:c0 + SW],
                                                  in_=ust[ub][8 * s4:8 * s4 + 8, :]).then_inc(sem_ust[ub], 16)
                            return ins
                        op = S.add("sp", pfn, reads=[r.ust[ub]], sem=sem_ust[ub], ndma=4)
                        out_final.append(op)
        for cc in range(4):
            for half in range(2):
                bank = next_bank(0, 4)

                def zfn(e, cc=cc, half=half, bank=bank, g=g):
                    ins = None
                    for jl in range(4):
                        j = half * 4 + jl
                        mslot = (4 + g) if j == 0 else g
                        o = PS[bank][:, jl * 128:(jl + 1) * 128]
                        e.matmul(out=o, lhsT=Ug[:, j, cc * 128:(cc + 1) * 128],
                                 rhs=BND[:, mslot * 128:(mslot + 1) * 128], start=True, stop=False)
                        hs = 8 + g * 8 + j
                        ins = e.matmul(out=o, lhsT=Ug[:, 8, cc * 128:(cc + 1) * 128],
                                       rhs=BND[:, hs * 128:(hs + 1) * 128], start=False, stop=True)
                    return ins
                S.add("pe", zfn, reads=[r.Ug, cres[9]], writes=[PSR[bank]])
                S.add("dve", (lambda cc, half, bank: lambda e: e.tensor_copy(
                    out=ZT[:, cc, half * 512:(half + 1) * 512], in_=PS[bank][:, :]))(cc, half, bank),
                    reads=[PSR[bank]], writes=[r.ZT])
            bank = next_bank(0, 4)

            def zsfn(e, cc=cc, bank=bank, g=g):
                o = PS[bank][:, 0:32]
                e.matmul(out=o, lhsT=Ug[0:32, 9, cc * 128:(cc + 1) * 128],
                         rhs=BNDS[0:32, g * 32:(g + 1) * 32], start=True, stop=False)
                c0 = g * 512 + cc * 128
                return e.matmul(out=o, lhsT=STb[0:64, c0:c0 + 128],
                                rhs=BNDS[0:64, 128 + g * 32:128 + (g + 1) * 32], start=False, stop=True)
            S.add("pe", zsfn, reads=[r.Ug, r.STb, cres[10]], writes=[PSR[bank]])
            S.add("dve", (lambda cc, bank: lambda e: e.tensor_copy(
                out=ZT[:, cc, 1024:NQ], in_=PS[bank][:, 0:32]))(cc, bank),
                reads=[PSR[bank]], writes=[r.ZT])
        for si in range(2):
            b = load_slab(w_in_v, 7168 + g * 512 + si * SW)
            for m in range(2):
                dc = si * 2 + m
                for (t0, n, o0) in ((0, 512, 0), (512, 512, 512), (1152, 32, 1024)):
                    bank = next_bank(0, 4)
                    mm_feat(b, m, t0, n, bank)
                    S.add("act", (lambda dc, o0, n, bank: lambda e: e.activation(
                        out=GPT[:, dc, o0:o0 + n], in_=PS[bank][:, 0:n], func=AF.Silu))(dc, o0, n, bank),
                        reads=[PSR[bank]], writes=[r.GPT])
        for dc in range(4):
            mb = (g * 4 + dc) % 2
            for (o0, n) in ((0, 512), (512, 512), (1024, 32)):
                bank = next_bank(0, 4)

                def pfn2(e, dc=dc, o0=o0, n=n, bank=bank, bw=bw):
                    ins = None
                    for cc in range(4):
                        ins = e.matmul(out=PS[bank][:, 0:n], lhsT=WP[bw][:, cc, dc * 128:(dc + 1) * 128],
                                       rhs=ZT[:, cc, o0:o0 + n], start=(cc == 0), stop=(cc == 3))
                    return ins
                S.add("pe", pfn2, reads=[r.WP[bw], r.ZT], writes=[PSR[bank]])
                S.add("dve", (lambda dc, o0, n, bank, mb, g: lambda e: e.scalar_tensor_tensor(
                    out=MPst[mb][:, o0:o0 + n], in0=PS[bank][:, 0:n],
                    scalar=pscale_t[:, g * 4 + dc:g * 4 + dc + 1], in1=GPT[:, dc, o0:o0 + n],
                    op0=ALU.mult, op1=ALU.mult))(dc, o0, n, bank, mb, g),
                    reads=[PSR[bank], r.GPT, cres[5]], writes=[r.MPst[mb]])
            ch = g * 4 + dc
            dma("sp", mixp[ch * 128:(ch + 1) * 128, :], MPst[mb], sem_mp[mb], reads=[r.MPst[mb]], writes=[r.mixp])
    S.barrier()

    for s8 in range(8):
        b = load_slab(w_in_v, 3072 + s8 * SW)
        for i in range(8):
            bank = next_bank(0, 4)
            mm_tok(b, i, bank)
            S.add("act", (lambda bank, i, s8: lambda e: e.activation(
                out=GA[:, i, s8 * SW:(s8 + 1) * SW], in_=PS[bank][:, 0:SW], func=AF.Silu))(bank, i, s8),
                reads=[PSR[bank]], writes=[r.GA[i]])
        for m in range(2):
            bank = next_bank(0, 4)
            mm_feat(b, m, 1152, 32, bank)
            h = s8 * 2 + m
            S.add("act", (lambda bank, h: lambda e: e.activation(
                out=gats[:, h * 32:(h + 1) * 32], in_=PS[bank][:, 0:32], func=AF.Silu))(bank, h),
                reads=[PSR[bank]], writes=[r.gats])
    for s8 in range(8):
        b = load_slab(w_in_v, s8 * SW)
        for m in range(2):
            h = s8 * 2 + m
            for k, (t0, n, o0) in enumerate(((0, 512, 0), (512, 512, 512), (1152, 32, 1024))):
                bank = next_bank(0, 4)
                mm_feat(b, m, t0, n, bank)
                wr = [r.QT[h][tt] for tt in (range(0, 4) if k == 0 else range(4, 8) if k == 1 else [8])]
                eng = "act" if k != 1 else "dve"
                if eng == "act":
                    S.add("act", (lambda bank, h, o0, n: lambda e: e.copy(
                        out=QT[:, h, o0:o0 + n], in_=PS[bank][:, 0:n]))(bank, h, o0, n),
                        reads=[PSR[bank]], writes=wr)
                else:
                    S.add("dve", (lambda bank, h, o0, n: lambda e: e.tensor_copy(
                        out=QT[:, h, o0:o0 + n], in_=PS[bank][:, 0:n]))(bank, h, o0, n),
                        reads=[PSR[bank]], writes=wr)
    S.barrier()

    r.qn = Res("qn")
    r.nbq = Res("nbq")
    r.qsq = Res("qsq")
    S.add("dve", lambda e: e.memset(qn[:, :], 0.0), writes=[r.qn])
    if DO_BOUND:
        for h in range(16):
            S.add("pool", (lambda h: lambda e: e.tensor_tensor(
                out=qsq[:, :], in0=QT[:, h, :], in1=QT[:, h, :], op=ALU.mult))(h),
                reads=[r.QT[h][t] for t in range(9)], writes=[r.qsq])

            def qnf(e, h=h):
                ins = None
                for t in range(8):
                    ins = e.matmul(out=PS[7][:, t * 16 + h:t * 16 + h + 1], lhsT=qsq[:, t * 128:(t + 1) * 128],
                                   rhs=ones_bf[:, 0:1], start=True, stop=True)
                return ins
            S.add("pe", qnf, reads=[r.qsq, cres[6]], writes=[PSR[7]])
        S.add("act", lambda e: e.activation(out=qn[:, 0:128], in_=PS[7][:, 0:128], func=AF.Sqrt),
              reads=[PSR[7]], writes=[r.qn])

    r.KT = [Res("KT0"), Res("KT1")]
    r.VV = [Res("VV0"), Res("VV1")]
    r.TH = [Res("TH0"), Res("TH1")]
    r.THst = Res("THst")
    r.kmb = Res("kmb")
    r.ks64 = Res("ks64")
    r.kmT = Res("kmT")
    r.sqc = Res("sqc")
    r.kmx = Res("kmx")
    r.nkmax = Res("nkmax")
    r.gm = Res("gm")
    r.mx8 = Res("mx8")
    r.selt = Res("selt")
    r.mrow = Res("mrow")
    r.mrT = [Res("mrT0"), Res("mrT1")]
    for i in range(2):
        S.add("dve", (lambda i: lambda e: e.memset(mrT[i][:, :], 0.0))(i), writes=[r.mrT[i]])
    r.pbuf = [Res(f"pbuf{i}") for i in range(4)]
    r.rsum = Res("rsum")
    r.yat = Res("yat")
    sem_kt = [newsem("kt0"), newsem("kt1")]
    sem_vv = [newsem("vv0"), newsem("vv1")]
    sem_th = newsem("th")

    kall_v = kall.ap().rearrange("(c h d) n -> d c h n", c=8, h=4)
    vall_v = vall.ap().rearrange("(c h r) n -> r c h n", c=8, h=4)

    def slot_of(sbk):
        return 8 * (sbk % 8) + sbk // 8

    def kmax_chunks(ktap, ncols, kres, extra=None):
        nch = ncols // 512
        idx = 0
        for cidx in range(nch):
            S.add("pool", (lambda cidx: lambda e: e.tensor_tensor(
                out=sqc[:, :], in0=ktap[:, cidx * 512:(cidx + 1) * 512],
                in1=ktap[:, cidx * 512:(cidx + 1) * 512], op=ALU.mult))(cidx),
                reads=[kres], writes=[r.sqc])
            S.add("pe", lambda e: e.matmul(out=PS[6][0:1, 0:512], lhsT=ones_bf[:, 0:1], rhs=sqc[:, :],
                                           start=True, stop=True),
                  reads=[r.sqc, cres[6]], writes=[PSR[6]])
            S.add("dve", (lambda cidx: lambda e: e.tensor_reduce(
                out=kmx[0:1, cidx:cidx + 1], in_=PS[6][0:1, 0:512], axis=AX.X, op=ALU.max))(cidx),
                reads=[PSR[6]], writes=[r.kmx])
            idx = cidx + 1
        if extra is not None:
            exap, exres, exn = extra
            S.add("pool", lambda e: e.tensor_tensor(out=sqc[:, 0:exn], in0=exap, in1=exap, op=ALU.mult),
                  reads=[exres], writes=[r.sqc])
            S.add("pe", lambda e: e.matmul(out=PS[6][0:1, 0:exn], lhsT=ones_bf[:, 0:1], rhs=sqc[:, 0:exn],
                                           start=True, stop=True),
                  reads=[r.sqc, cres[6]], writes=[PSR[6]])
            S.add("dve", (lambda idx: lambda e: e.tensor_reduce(
                out=kmx[0:1, idx:idx + 1], in_=PS[6][0:1, 0:exn], axis=AX.X, op=ALU.max))(idx),
                reads=[PSR[6]], writes=[r.kmx])
            idx += 1
        S.add("dve", (lambda idx: lambda e: e.tensor_reduce(
            out=kmx1[0:1, 0:1], in_=kmx[0:1, 0:idx], axis=AX.X, op=ALU.max))(idx),
            reads=[r.kmx], writes=[r.kmx])
        S.add("act", lambda e: e.activation(out=kmx1[0:1, 1:2], in_=kmx1[0:1, 0:1], func=AF.Sqrt, scale=1.05),
              reads=[r.kmx], writes=[r.kmx])
        S.add("dve", lambda e: e.tensor_copy(out=sqc[0:1, 0:1], in_=kmx1[0:1, 1:2]),
              reads=[r.kmx], writes=[r.sqc])
        S.add("pe", lambda e: e.matmul(out=PS[6][:, 0:1], lhsT=ones_bf[0:1, :], rhs=sqc[0:1, 0:1],
                                       start=True, stop=True),
              reads=[r.sqc, cres[6]], writes=[PSR[6]])
        S.add("act", lambda e: e.activation(out=nkmax[:, :], in_=PS[6][:, 0:1], func=AF.Copy, scale=-1.0),
              reads=[PSR[6]], writes=[r.nkmax])

    def attn_pipeline(groups, q_rhs, nq, kt_fn, sel_fn, mrT_ap, mrk, v_fn, e_fn, obank, reads_q, reads_kv,
                      pcols, first_start=True, last_stop=True, sbanks=(0, 1, 2, 6), hook1=None, hook2=None):
        pend = []
        ng = len(groups)
        for gi, grp in enumerate(groups):
            bank = sbanks[gi % len(sbanks)]
            pb = gi % 4
            width = pcols * len(grp)

            def sfn(e, grp=grp, bank=bank):
                ins = None
                for il, kb in enumerate(grp):
                    o = PS[bank][:, il * pcols:il * pcols + nq]
                    e.matmul(out=o, lhsT=kt_fn(kb), rhs=q_rhs, start=True, stop=False)
                    ins = e.matmul(out=o, lhsT=sel_fn(kb), rhs=mrT_ap, start=False, stop=True)
                return ins
            S.add("pe", sfn, reads=reads_q + reads_kv + [mrk, cres[2]], writes=[PSR[bank]])
            S.add("act", (lambda bank, pb, width: lambda e: e.activation(
                out=pbuf[pb][:, 0:width], in_=PS[bank][:, 0:width], func=AF.Exp, scale=SCALE))(bank, pb, width),
                reads=[PSR[bank]], writes=[r.pbuf[pb]])
            eap, eres = e_fn(gi, grp)
            pv_ = pbuf[pb][:, 0:width]
            if len(eap.shape) == 3:
                pv_ = pv_.rearrange("p (a b) -> p a b", a=eap.shape[1])
            S.add("pool", (lambda pv_, eap: lambda e: e.tensor_tensor(
                out=pv_, in0=pv_, in1=eap, op=ALU.mult))(pv_, eap),
                reads=[r.pbuf[pb]] + eres, writes=[r.pbuf[pb]])

            def pvfn(e, grp=grp, pb=pb, gi=gi):
                ins = None
                for il, kb in enumerate(grp):
                    ins = e.matmul(out=PS[obank][0:nq, 0:129], lhsT=pbuf[pb][:, il * pcols:il * pcols + nq],
                                   rhs=v_fn(kb),
                                   start=(first_start and gi == 0 and il == 0),
                                   stop=(last_stop and gi == ng - 1 and il == len(grp) - 1))
                return ins
            if gi == 0 and hook1 is not None:
                hook1()
            if len(pend) == 2:
                p0 = pend.pop(0)
                S.add("pe", p0[0], reads=p0[1], writes=[PSR[obank]])
            pend.append((pvfn, [r.pbuf[pb]] + reads_kv))
        if hook2 is not None:
            hook2()
        for p0 in pend:
            S.add("pe", p0[0], reads=p0[1], writes=[PSR[obank]])

    if DO_PATT:
        ldctr = [0]

        def load_kv(kvh):
            b = ldctr[0] % 2
            ldctr[0] += 1
            dma("sp", KT[b].rearrange("p (c n) -> p c n", c=8), kall_v[:, :, kvh, :], sem_kt[b],
                reads=[r.kall], writes=[r.KT[b]])
            dma("sp", VV[b].rearrange("p s c -> p (s c)").rearrange("p (c n) -> p c n", c=8),
                vall_v[:, :, kvh, :], sem_vv[b], reads=[r.vall], writes=[r.VV[b]])
            return b

        thctr = [0]

        def load_table(h):
            b = thctr[0] % 2
            thctr[0] += 1
            dma("sp", THst, bass.AP(gext, h * GEXT, [[1, 128], [1, TL]]), sem_th, writes=[r.THst])
            S.add("act", (lambda b: lambda e: e.activation(out=TH[b], in_=THst, func=AF.Exp))(b),
                  reads=[r.THst], writes=[r.TH[b]])
            return b

        nextb = load_kv(0)
        octr = 0
        for kvh in range(4):
            b = nextb
            if kvh < 3:
                nextb = load_kv(kvh + 1)
            S.add("dve", (lambda b: lambda e: e.tensor_reduce(
                out=ks64[:, :], in_=KT[b].rearrange("p (s n) -> p s n", s=64), axis=AX.X, op=ALU.add))(b),
                reads=[r.KT[b]], writes=[r.ks64])
            S.add("dve", lambda e: e.tensor_tensor(
                out=kmT[:, :].rearrange("p (j c) -> p c j", c=4),
                in0=ks64[:, :].rearrange("p (c a j) -> p c a j", c=4, a=2)[:, :, 0, :],
                in1=ks64[:, :].rearrange("p (c a j) -> p c a j", c=4, a=2)[:, :, 1, :], op=ALU.add),
                reads=[r.ks64], writes=[r.kmT])
            S.add("dve", lambda e: e.tensor_scalar(out=kmb[:, :], in0=kmT[:, :], scalar1=1.0 / 256, scalar2=None,
                                                   op0=ALU.mult),
                  reads=[r.kmT], writes=[r.kmb])
            if DO_BOUND:
                kmax_chunks(KT[b], 8192, r.KT[b])
                S.add("dve", lambda e: e.tensor_scalar(out=nbq[:, 0:128], in0=qn[:, 0:128], scalar1=nkmax[:, 0:1],
                                                       scalar2=NEG, op0=ALU.mult, op1=ALU.add),
                      reads=[r.qn, r.nkmax], writes=[r.nbq])
            else:
                S.add("dve", lambda e: e.memset(nbq[:, :], NEG), writes=[r.nbq])
            items = [(hg, j) for hg in range(4) for j in range(8)]
            tbs = {}
            mctr = [0]

            def mask_stage1(h, j):
                qres = r.QT[h][j]
                q_ap = QT[:, h, j * 128:(j + 1) * 128]
                S.add("pe", (lambda q_ap: lambda e: e.matmul(out=PS[5][:, 0:32], lhsT=q_ap, rhs=kmb[:, :],
                                                             start=True, stop=True))(q_ap),
                      reads=[qres, r.kmb], writes=[PSR[5]])
                S.add("dve", (lambda j: lambda e: e.tensor_tensor(
                    out=gm[:, :], in0=PS[5][:, 0:32], in1=bmask_t[:, j * 32:(j + 1) * 32], op=ALU.add))(j),
                    reads=[PSR[5], cres[4]], writes=[r.gm])
                S.add("dve", lambda e: e.max(out=mx8[:, :], in_=gm[:, :]), reads=[r.gm], writes=[r.mx8])
                S.add("dve", lambda e: e.tensor_scalar(out=selt[:, :], in0=gm[:, :], scalar1=mx8[:, 3:4],
                                                       scalar2=None, op0=ALU.is_ge),
                      reads=[r.gm, r.mx8], writes=[r.selt])
                col = j * 16 + h
                S.add("dve", (lambda col: lambda e: e.tensor_scalar(
                    out=mrow[:, 0:32], in0=selt[:, :], scalar1=-NEG, scalar2=nbq[:, col:col + 1],
                    op0=ALU.mult, op1=ALU.add))(col),
                    reads=[r.selt, r.nbq], writes=[r.mrow])

            def mask_stage2():
                mi = mctr[0] % 2
                mctr[0] += 1
                S.add("pe", lambda e: e.transpose(out=psb(5)[0:32, 128:256], in_=mrow[:, 0:32],
                                                  identity=ident[:, :]),
                      reads=[r.mrow, cres[0]], writes=[PSR[5]])
                S.add("act", (lambda mi: lambda e: e.copy(out=mrT[mi][0:32, :], in_=psb(5)[0:32, 128:256]))(mi),
                      reads=[PSR[5]], writes=[r.mrT[mi]])
                return mi

            pend_ep = [None]

            def flush_ep():
                if pend_ep[0] is not None:
                    q_ap_, qres_ = pend_ep[0]
                    S.add("pe", lambda e: e.transpose(out=psb(5)[:, 256:384], in_=yat[:, :], identity=ident[:, :]),
                          reads=[r.yat, cres[0]], writes=[PSR[5]])
                    S.add("act", (lambda q_ap_: lambda e: e.copy(out=q_ap_, in_=psb(5)[:, 256:384]))(q_ap_),
                          reads=[PSR[5]], writes=[qres_])
                    pend_ep[0] = None

            tbs[kvh * 4] = load_table(kvh * 4)
            mask_stage1(kvh * 4, 0)
            cur_mi = mask_stage2()
            for idx, (hg, j) in enumerate(items):
                h = kvh * 4 + hg
                if j == 0 and hg < 3:
                    tbs[h + 1] = load_table(h + 1)
                tb = tbs[h]
                mi = cur_mi
                qres = r.QT[h][j]
                q_ap = QT[:, h, j * 128:(j + 1) * 128]
                nxt = items[idx + 1] if idx + 1 < len(items) else None
                nsb = 8 * j + 8
                groups = []
                for g0 in range(0, nsb, 4):
                    groups.append([(i_, 8 * j + 7 - i_) for i_ in range(g0, g0 + 4)])
                ob = 3 + (octr % 2)
                octr += 1

                def e_fn(gi, grp, tb=tb):
                    i0 = grp[0][0]
                    if i0 + 3 <= 31:
                        return TH[tb][:, i0 * 128:i0 * 128 + 512], [r.TH[tb]]
                    base = TH[tb][:, 31 * 128:32 * 128]
                    return bass.AP(base.tensor, base.offset, [list(base.ap[0]), [0, 4], [1, 128]]), [r.TH[tb]]
                nmi = [None]

                def hook1(nxt=nxt):
                    flush_ep()
                    if nxt is not None:
                        mask_stage1(kvh * 4 + nxt[0], nxt[1])

                def hook2(nxt=nxt, nmi=nmi):
                    if nxt is not None:
                        nmi[0] = mask_stage2()
                attn_pipeline(
                    groups, q_ap, 128,
                    kt_fn=lambda kb, b=b: KT[b][:, slot_of(kb[1]) * 128:(slot_of(kb[1]) + 1) * 128],
                    sel_fn=lambda kb: sel[:, (kb[1] // 2) * 128:(kb[1] // 2 + 1) * 128],
                    mrT_ap=mrT[mi][:, :], mrk=r.mrT[mi],
                    v_fn=lambda kb, b=b: VV[b][:, slot_of(kb[1]), 0:129],
                    e_fn=e_fn, obank=ob, reads_q=[qres], reads_kv=[r.KT[b], r.VV[b]], pcols=128,
                    hook1=hook1, hook2=hook2)
                cur_mi = nmi[0]
                S.add("dve", (lambda ob: lambda e: e.reciprocal(out=rsum[:, :], in_=PS[ob][:, 128:129]))(ob),
                      reads=[PSR[ob]], writes=[r.rsum])
                S.add("dve", (lambda ob, j, h: lambda e: e.scalar_tensor_tensor(
                    out=yat[:, :], in0=PS[ob][:, 0:128], scalar=rsum[:, 0:1],
                    in1=GA[:, j, h * 128:(h + 1) * 128], op0=ALU.mult, op1=ALU.mult))(ob, j, h),
                    reads=[PSR[ob], r.rsum, r.GA[j]], writes=[r.yat])
                pend_ep[0] = (q_ap, qres)
            flush_ep()
    S.barrier()

    if DO_SATT:
        r.KST = [Res("KST0"), Res("KST1")]
        r.KB = Res("KB")
        r.KTs = Res("KTs")
        r.VBc = Res("VBc")
        r.Ps = [Res(f"Ps{k}") for k in range(4)]
        r.Es = Res("Es")
        r.idx = Res("idx")
        r.En = Res("En")
        r.dgs = Res("dgs")
        sem_kst2 = [newsem("kstg0"), newsem("kstg1")]
        sem_es = newsem("es")
        En_st = sb("Enst", [32, 512], F32)
        En_b = sb("Enb", [32, 512], BF16)
        load_const(ptab_t[:, :], bass.AP(ptab, 0, [[0, 128], [1, 256]]), r.idx)
        S.add("dve", lambda e: e.tensor_copy(out=idxf[:, :], in_=ptab_t[:, :]), reads=[r.idx], writes=[r.idx])
        S.add("dve", lambda e: e.tensor_scalar(out=idxf[:, :], in0=idxf[:, :], scalar1=128.0, scalar2=iota_p[:, 0:1],
                                               op0=ALU.mult, op1=ALU.add),
              reads=[r.idx, cres[3]], writes=[r.idx])
        S.add("dve", lambda e: e.tensor_copy(out=idxi[:, :], in_=idxf[:, :]), reads=[r.idx], writes=[r.idx])
        load_const(En_st[:, :], rbn.ap(), r.En)
        S.add("act", lambda e: e.activation(out=En_b[:, :], in_=En_st[:, :], func=AF.Exp), reads=[r.En], writes=[r.En])
        S.add("dve", lambda e: e.memset(VBc.rearrange("p g h c -> p (g h c)"), 0.0), writes=[r.VBc])
        S.add("dve", lambda e: e.memset(VBc[:, :, :, 128:129], 1.0), writes=[r.VBc])

        gctr = [0]

        def gather_chunk(cache, s4, ch):
            b = gctr[0] % 2
            gctr[0] += 1

            def fn(e, b=b):
                ins = None
                for pg in range(4):
                    ci = s4 * 64 + ch * 4 + pg
                    ins = e.indirect_dma_start(
                        out=KST[b][:, pg, :], out_offset=None, in_=cache[:, :],
                        in_offset=bass.IndirectOffsetOnAxis(ap=idxi[:, ci:ci + 1], axis=0)).then_inc(sem_kst2[b], 16)
                return ins
            S.add("pool", fn, reads=[r.idx], writes=[r.KST[b]], sem=sem_kst2[b], ndma=4)
            return b

        for s4 in range(4):
            S.add("dve", (lambda s4: lambda e: e.tensor_copy(
                out=qs[:, :].rearrange("p (h t) -> p h t", h=16),
                in_=QT[:, :, 1024 + 8 * s4:1024 + 8 * s4 + 8]))(s4),
                reads=[r.QT[h_][8] for h_ in range(16)], writes=[r_qs])
            for ch in range(16):
                b = gather_chunk(cache_k, s4, ch)
                S.add("act", (lambda b: lambda e: e.copy(out=KB.rearrange("p g c -> p (g c)"),
                                                         in_=KST[b].rearrange("p g c -> p (g c)")))(b),
                      reads=[r.KST[b]], writes=[r.KB])
                for p2 in range(2):
                    bank = p2 % 2

                    def tfn2(e, p2=p2, bank=bank):
                        ins = None
                        pv = psb(bank).rearrange("p (h g n) -> p h g n", h=4, g=2)
                        for hh in range(4):
                            for gg in range(2):
                                ins = e.transpose(out=pv[:, hh, gg, :],
                                                  in_=KB[:, p2 * 2 + gg, hh * 128:(hh + 1) * 128],
                                                  identity=ident[:, :])
                        return ins
                    S.add("pe", tfn2, reads=[r.KB, cres[0]], writes=[PSR[bank]])
                    pg0 = ch * 4 + p2 * 2
                    ceng = "dve" if p2 % 2 == 0 else "act"

                    def cfn2(e, bank=bank, pg0=pg0, ceng=ceng):
                        o = KTs[:, :, pg0 * 128:(pg0 + 2) * 128]
                        i_ = psb(bank).rearrange("p (h n) -> p h n", h=4)
                        if ceng == "act":
                            return e.copy(out=o, in_=i_)
                        return e.tensor_copy(out=o, in_=i_)
                    S.add(ceng, cfn2, reads=[PSR[bank]], writes=[r.KTs])
            for kvh in range(4):
                S.add("dve", (lambda kvh: lambda e: e.tensor_reduce(
                    out=kmT[:, :], in_=KTs[:, kvh, :].rearrange("p (s n) -> p s n", s=32), axis=AX.X,
                    op=ALU.add))(kvh), reads=[r.KTs], writes=[r.kmT])
                S.add("dve", lambda e: e.tensor_scalar(out=kmb[:, :], in0=kmT[:, :], scalar1=1.0 / 256, scalar2=None,
                                                       op0=ALU.mult), reads=[r.kmT], writes=[r.kmb])
                q_ap = qs[:, kvh * 32:(kvh + 1) * 32]
                qres = [r_qs]
                if DO_BOUND:
                    kmax_chunks(KTs[:, kvh, :], 8192, r.KTs,
                                extra=(ktnew[:, kvh * 32 + 8 * s4:kvh * 32 + 8 * s4 + 8], r.ktnew, 8))
                    S.add("pool", (lambda q_ap: lambda e: e.tensor_tensor(
                        out=qsq[:, 0:32], in0=q_ap, in1=q_ap, op=ALU.mult))(q_ap),
                        reads=qres, writes=[r.qsq])
                    S.add("pe", lambda e: e.matmul(out=PS[7][0:32, 0:1], lhsT=qsq[:, 0:32], rhs=ones_bf[:, 0:1],
                                                   start=True, stop=True),
                          reads=[r.qsq, cres[6]], writes=[PSR[7]])
                    S.add("act", lambda e: e.activation(out=qn[0:32, 128:129], in_=PS[7][0:32, 0:1], func=AF.Sqrt),
                          reads=[PSR[7]], writes=[r.qn])
                    S.add("dve", lambda e: e.tensor_scalar(out=nbq[0:32, 128:129], in0=qn[0:32, 128:129],
                                                           scalar1=nkmax[0:32, 0:1], scalar2=None, op0=ALU.mult),
                          reads=[r.qn, r.nkmax], writes=[r.nbq])
                else:
                    S.add("dve", lambda e: e.memset(nbq[0:32, 128:129], 0.0), writes=[r.nbq])
                S.add("pe", (lambda q_ap: lambda e: e.matmul(out=PS[5][0:32, 0:32], lhsT=q_ap, rhs=kmb[:, :],
                                                             start=True, stop=True))(q_ap),
                      reads=qres + [r.kmb], writes=[PSR[5]])
                S.add("dve", lambda e: e.tensor_copy(out=gm[0:32, :], in_=PS[5][0:32, 0:32]),
                      reads=[PSR[5]], writes=[r.gm])
                S.add("dve", lambda e: e.max(out=mx8[0:32, :], in_=gm[0:32, :]), reads=[r.gm], writes=[r.mx8])
                S.add("dve", lambda e: e.tensor_scalar(out=selt[0:32, :], in0=gm[0:32, :], scalar1=mx8[0:32, 2:3],
                                                       scalar2=None, op0=ALU.is_ge),
                      reads=[r.gm, r.mx8], writes=[r.selt])
                S.add("dve", lambda e: e.tensor_scalar(out=selt[0:32, :], in0=selt[0:32, :], scalar1=-NEG, scalar2=NEG,
                                                       op0=ALU.mult, op1=ALU.add),
                      reads=[r.selt], writes=[r.selt])
                S.add("dve", lambda e: e.tensor_scalar(out=mrow[0:32, 0:32], in0=selt[0:32, :],
                                                       scalar1=nbq[0:32, 128:129], scalar2=None, op0=ALU.add),
                      reads=[r.selt, r.nbq], writes=[r.mrow])
                S.add("dve", lambda e: e.tensor_copy(out=mrow[0:32, 32:33], in_=nbq[0:32, 128:129]),
                      reads=[r.nbq, r.mrow], writes=[r.mrow])
                S.add("pe", lambda e: e.transpose(out=psb(5)[0:33, 128:160], in_=mrow[0:32, 0:33],
                                                  identity=ident[0:32, 0:32]),
                      reads=[r.mrow, cres[0]], writes=[PSR[5]])
                S.add("act", lambda e: e.copy(out=mrT[0][0:33, 0:32], in_=psb(5)[0:33, 128:160]),
                      reads=[PSR[5]], writes=[r.mrT[0]])
                dma("sp", Esst, rbs[kvh * 128:(kvh + 1) * 128, :], sem_es, writes=[r.Es])
                S.add("act", lambda e: e.activation(out=Es, in_=Esst, func=AF.Exp), reads=[r.Es], writes=[r.Es])
                for bq in range(4):
                    bank = bq

                    def sfn2(e, bq=bq, bank=bank, kvh=kvh, q_ap=q_ap):
                        ins = None
                        for pl in range(16):
                            pg = bq * 16 + pl
                            o = PS[bank][:, pl * 32:(pl + 1) * 32]
                            e.matmul(out=o, lhsT=KTs[:, kvh, pg * 128:(pg + 1) * 128], rhs=q_ap, start=True, stop=False)
                            n = pg // 2
                            ins = e.matmul(out=o, lhsT=sel[:, n * 128:(n + 1) * 128], rhs=mrT[0][:, 0:32],
                                           start=False, stop=True)
                        return ins
                    S.add("pe", sfn2, reads=qres + [r.KTs, r.mrT[0], cres[2]], writes=[PSR[bank]])
                    pa = Ps_all[:, kvh, bq * 16:(bq + 1) * 16, :].rearrange("p g q -> p (g q)")
                    S.add("act", (lambda bank, pa: lambda e: e.activation(out=pa, in_=PS[bank][:, :], func=AF.Exp,
                                                                          scale=SCALE))(bank, pa),
                          reads=[PSR[bank]], writes=[r.Ps[kvh]])
                    S.add("pool", (lambda pa, bq: lambda e: e.tensor_tensor(
                        out=pa, in0=pa, in1=Es[:, bq * 512:(bq + 1) * 512], op=ALU.mult))(pa, bq),
                        reads=[r.Ps[kvh], r.Es], writes=[r.Ps[kvh]])

                def snew(e, kvh=kvh, q_ap=q_ap):
                    o = PS[5][0:32, 256:288]
                    e.matmul(out=o, lhsT=ktnew[:, kvh * 32:(kvh + 1) * 32], rhs=q_ap, start=True, stop=False)
                    return e.matmul(out=o, lhsT=sel[:, 32 * 128:32 * 128 + 32], rhs=mrT[0][:, 0:32],
                                    start=False, stop=True)
                S.add("pe", snew, reads=qres + [r.ktnew, r.mrT[0], cres[2]], writes=[PSR[5]])
                pn = pbuf[kvh % 3]
                S.add("act", (lambda pn: lambda e: e.activation(out=pn[0:32, 0:32],
                                                                in_=PS[5][0:32, 256:288], func=AF.Exp, scale=SCALE))(pn),
                      reads=[PSR[5]], writes=[r.pbuf[kvh % 3]])
                ecol = (kvh * 4 + s4) * 32
                S.add("pool", (lambda pn, ecol: lambda e: e.tensor_tensor(
                    out=pn[0:32, 0:32], in0=pn[0:32, 0:32], in1=En_b[0:32, ecol:ecol + 32], op=ALU.mult))(pn, ecol),
                    reads=[r.pbuf[kvh % 3], r.En], writes=[r.pbuf[kvh % 3]])
                S.add("dve", (lambda pn, kvh: lambda e: e.tensor_copy(
                    out=pnew[0:32, kvh * 32:(kvh + 1) * 32], in_=pn[0:32, 0:32]))(pn, kvh),
                    reads=[r.pbuf[kvh % 3]], writes=[r_pnew])
            for ch in range(16):
                b = gather_chunk(cache_v, s4, ch)
                S.add("act", (lambda b: lambda e: e.copy(
                    out=VBc[:, :, :, 0:128], in_=KST[b].rearrange("p g (h c) -> p g h c", h=4)))(b),
                    reads=[r.KST[b]], writes=[r.VBc])
                for kvh in range(4):
                    def pv2(e, kvh=kvh, ch=ch):
                        ins = None
                        for pg in range(4):
                            ins = e.matmul(out=PS[4 + kvh][0:32, 0:129], lhsT=Ps_all[:, kvh, ch * 4 + pg, :],
                                           rhs=VBc[:, pg, kvh, 0:129], start=(ch == 0 and pg == 0), stop=False)
                        return ins
                    S.add("pe", pv2, reads=[r.Ps[kvh], r.VBc], writes=[PSR[4 + kvh]])
            for kvh in range(4):
                S.add("pe", (lambda kvh: lambda e: e.matmul(
                    out=PS[4 + kvh][0:32, 0:129], lhsT=pnew[0:32, kvh * 32:(kvh + 1) * 32],
                    rhs=vbnew[0:32, kvh * 130:kvh * 130 + 129], start=False, stop=True))(kvh),
                    reads=[r_pnew, r.vbnew], writes=[PSR[4 + kvh]])
                S.add("dve", (lambda kvh: lambda e: e.reciprocal(out=rsum[0:32, :], in_=PS[4 + kvh][0:32, 128:129]))(kvh),
                      reads=[PSR[4 + kvh]], writes=[r.rsum])
                S.add("dve", (lambda kvh: lambda e: e.tensor_scalar(
                    out=yat[0:32, :], in0=PS[4 + kvh][0:32, 0:128], scalar1=rsum[0:32, 0:1], scalar2=None,
                    op0=ALU.mult))(kvh), reads=[PSR[4 + kvh], r.rsum], writes=[r.yat])
                S.add("pe", lambda e: e.transpose(out=psb(0)[:, 0:32], in_=yat[0:32, :], identity=ident[0:32, 0:32]),
                      reads=[r.yat, cres[0]], writes=[PSR[0]])
                S.add("dve", (lambda kvh, s4: lambda e: e.tensor_tensor(
                    out=QT[:, 4 * kvh:4 * kvh + 4, 1024 + 8 * s4:1024 + 8 * s4 + 8],
                    in0=psb(0)[:, 0:32].rearrange("p (g t) -> p g t", g=4),
                    in1=gats[:, :].rearrange("p (h t) -> p h t", h=16)[:, 4 * kvh:4 * kvh + 4, 8 * s4:8 * s4 + 8],
                    op=ALU.mult))(kvh, s4),
                    reads=[PSR[0], r.gats], writes=[r.QT[4 * kvh + g_][8] for g_ in range(4)])
    S.barrier()

    r.MP = Res("MP")
    r.xsl = Res("xsl")
    r.hst = [Res("hst0"), Res("hst1")]
    r.ssq2 = Res("ssq2")
    r.y = [Res(f"y{t}") for t in range(9)]
    r.ybuf = [Res("ybuf0"), Res("ybuf1")]
    r.gout = Res("gout")
    sem_mpl = newsem("mpl")
    sem_xsl = newsem("xsl")
    sem_hst = [newsem("hst0"), newsem("hst1")]
    sem_yb = [newsem("yb0"), newsem("yb1")]
    hst = ust
    dma("sp", MP, mixp.ap().rearrange("(c p) t -> p c t", p=128), sem_mpl, reads=[r.mixp], writes=[r.MP])
    load_const(gout_t, bass.AP(gain_out, 0, [[0, 128], [1, D]]), r.gout)
    hctr = 0
    for so in range(16):
        b = load_slab(w_out_v, so * SW)
        c0 = so * SW
        dma("sp", xsl[:, 0:8, :], xall[0:1024, c0:c0 + SW].rearrange("(t p) c -> p t c", p=128), sem_xsl,
            writes=[r.xsl])
        dma("sp", xsl[0:32, 8, :], xall[1152:1184, c0:c0 + SW], sem_xsl, writes=[r.xsl])
        for t in range(9):
            rows = 32 if t == 8 else 128
            bank = next_bank(0, 4)

            def ofn(e, t=t, rows=rows, bank=bank, b=b):
                ins = None
                q0 = t * 128
                for kc in range(32):
                    lt = QT[:, kc, q0:q0 + rows] if kc < 16 else MP[:, kc - 16, q0:q0 + rows]
                    ins = e.matmul(out=PS[bank][0:rows, 0:SW], lhsT=lt, rhs=slab[b][:, kc, :],
                                   start=(kc == 0), stop=(kc == 31))
                return ins
            S.add("pe", ofn, reads=[r.QT[h][t] for h in range(16)] + [r.MP, r.slab[b]], writes=[PSR[bank]])
            hb = hctr % 2
            hctr += 1
            S.add("dve", (lambda rows, bank, hb, t: lambda e: e.tensor_tensor(
                out=hst[hb][0:rows, :], in0=PS[bank][0:rows, 0:SW], in1=xsl[0:rows, t, :], op=ALU.add))(rows, bank, hb, t),
                reads=[PSR[bank], r.xsl], writes=[r.hst[hb]])
            S.add("act", (lambda rows, hb, t, so: lambda e: e.activation(
                out=junk2[0:rows, :], in_=hst[hb][0:rows, :], func=AF.Square,
                accum_out=ssq2[0:rows, t * 16 + so:t * 16 + so + 1]))(rows, hb, t, so),
                reads=[r.hst[hb]], writes=[r.ssq2])
            dma("sp", y_o[t * 128:t * 128 + rows, c0:c0 + SW], hst[hb][0:rows, :], sem_hst[hb],
                reads=[r.hst[hb]], writes=[r.y[t]])
    for t in range(9):
        rows = 32 if t == 8 else 128
        yb = t % 2
        dma("sp", ybuf[yb][0:rows, :], y_o[t * 128:t * 128 + rows, :], sem_yb[yb], reads=[r.y[t]], writes=[r.ybuf[yb]])
        S.add("dve", (lambda rows, t: lambda e: e.tensor_reduce(
            out=rstd[0:rows, t:t + 1], in_=ssq2[0:rows, t * 16:(t + 1) * 16], axis=AX.X, op=ALU.add))(rows, t),
            reads=[r.ssq2], writes=[r.rstd])
        S.add("act", (lambda rows, t: lambda e: e.activation(
            out=rstd[0:rows, t:t + 1], in_=rstd[0:rows, t:t + 1], func=AF.Sqrt, bias=eps_t[0:rows, :],
            scale=1.0 / D))(rows, t), reads=[r.rstd, cres[7]], writes=[r.rstd])
        S.add("dve", (lambda rows, t: lambda e: e.reciprocal(out=rstd[0:rows, t:t + 1], in_=rstd[0:rows, t:t + 1]))(rows, t),
              reads=[r.rstd], writes=[r.rstd])
        S.add("dve", (lambda rows, t, yb: lambda e: e.scalar_tensor_tensor(
            out=ybuf[yb][0:rows, :], in0=ybuf[yb][0:rows, :], scalar=rstd[0:rows, t:t + 1], in1=gout_t[0:rows, :],
            op0=ALU.mult, op1=ALU.mult))(rows, t, yb), reads=[r.ybuf[yb], r.rstd, r.gout], writes=[r.ybuf[yb]])
        op = dma("sp", y_o[t * 128:t * 128 + rows, :], ybuf[yb][0:rows, :], sem_yb[yb], reads=[r.ybuf[yb]],
                 writes=[r.y[t]])
        out_final.append(op)
    fin = Op("sp", None)
    fin.deps = set(out_final)
    S.ops["sp"].append(fin)
    S.all.append(fin)

    S.finalize(esem)
    with nc.Block() as block:
        @block.tensor
        def _(e):
            S.emit("pe", e)

        @block.scalar
        def _(e):
            S.emit("act", e)

        @block.vector
        def _(e):
            S.emit("dve", e)

        @block.gpsimd
        def _(e):
            S.emit("pool", e)

        @block.sync
        def _(e):
            S.emit("sp", e)
    es.close()
    return nc


def _bucket_table(nmax):
    def _np():
        n = np.arange(nmax, dtype=np.int32)
        nf = np.maximum(n, 1).astype(np.float32)
        large = 16 + (np.log(nf / np.float32(16)) / np.float32(math.log(4096 / 16))
                      * np.float32(32 - 16)).astype(np.int32)
        large = np.minimum(large, 31)
        return np.where(n < 16, n, large)
    try:
        with jax.default_device(jax.devices("cpu")[0]):
            n = jnp.arange(nmax, dtype=jnp.int32)
            nf = jnp.maximum(n, 1).astype(jnp.float32)
            large = 16 + (jnp.log(nf / 16) / math.log(4096 / 16) * (32 - 16)).astype(jnp.int32)
            large = jnp.minimum(large, 31)
            return np.asarray(jnp.where(n < 16, n, large))
    except Exception:
        return _np()


_NC_CACHE = {}


def kernel(x_prompt, x_sample, cache_k, cache_v, state_pool, page_table, norm_in, w_in, w_pool, pool_scale,
           w_out, rel_bias, norm_out):
    bf = ml_dtypes.bfloat16
    f32 = np.float32
    xp = np.asarray(x_prompt, f32).reshape(SEQ, D)
    xs = np.asarray(x_sample, f32).reshape(32, 8, D)
    rel_bias = np.asarray(rel_bias, f32)
    bk = _bucket_table(8400)

    def gvals(dist):
        d = np.asarray(dist)
        out = np.full(d.shape + (16,), NEG, f32)
        ok = d >= 0
        out[ok] = rel_bias[bk[d[ok]]]
        return out

    ck = np.ascontiguousarray(np.asarray(cache_k, f32).reshape(2560 * 128, 512))
    cv = np.ascontiguousarray(np.asarray(cache_v, f32).reshape(2560 * 128, 512))
    w_in2 = np.ascontiguousarray(np.asarray(w_in, f32).reshape(D, INW))
    w_out2 = np.ascontiguousarray(np.asarray(w_out, f32).reshape(D, D))
    w_pool2 = np.ascontiguousarray(np.asarray(w_pool, f32).reshape(4 * 512, 512))
    gain_in = np.asarray(norm_in, f32).reshape(1, D)
    gain_out = np.asarray(norm_out, f32).reshape(1, D)
    pscale = np.ascontiguousarray(np.asarray(pool_scale, f32).reshape(16, 128).T)
    ident = np.eye(128, dtype=f32)
    selm = np.zeros((128, 33 * 128), f32)
    for n in range(32):
        selm[n, n * 128:(n + 1) * 128] = 1.0
    selm[32, 32 * 128:33 * 128] = 1.0
    iota = np.arange(128, dtype=f32).reshape(128, 1)

    k_ = np.arange(128)[:, None, None, None]
    pg_ = np.arange(64)[None, :, None, None]
    tok_ = np.arange(8)[None, None, None, :]
    dist = np.broadcast_to(8192 + tok_ - 128 * pg_ - k_, (128, 64, 1, 8))
    gv = gvals(dist[:, :, 0, :])
    rbs = np.zeros((4, 128, 64, 4, 8), f32)
    for kvh in range(4):
        for g in range(4):
            rbs[kvh, :, :, g, :] = gv[:, :, :, 4 * kvh + g]
    rbs = rbs.reshape(4 * 128, 2048)
    tk = np.arange(8)[:, None]
    tq = np.arange(8)[None, :]
    gn = gvals(tq - tk)
    rbn = np.full((32, 4, 4, 4, 8), NEG, f32)
    for s2 in range(4):
        for kvh in range(4):
            for g in range(4):
                rbn[8 * s2:8 * s2 + 8, kvh, s2, g, :] = gn[:, :, 4 * kvh + g]
    rbn = rbn.reshape(32, 512)

    wins = (2, 4, 8, 16)
    tp = np.arange(128)[:, None]
    t = np.arange(128)[None, :]
    bands_base = np.zeros((40, 128, 128), f32)
    for g, w in enumerate(wins):
        inwin = ((t - tp) >= 0) & ((t - tp) < w)
        bands_base[g] = inwin / w - (tp == t)
        cnt = np.minimum(w, t + 1)
        bands_base[4 + g] = inwin / cnt - (tp == t)
        for j in range(8):
            hb = np.zeros((128, 128), f32)
            i = np.arange(16)[:, None]
            hb[16 * j:16 * j + 16, :] = ((t + 16 - i) < w) / w
            bands_base[8 + g * 8 + j] = hb
    bands_s = np.zeros((64, 256), f32)
    for g, w in enumerate(wins):
        for s2 in range(4):
            tk8 = np.arange(8)[:, None]
            t8 = np.arange(8)[None, :]
            inw = ((t8 - tk8) >= 0) & ((t8 - tk8) < w)
            bands_s[8 * s2:8 * s2 + 8, g * 32 + 8 * s2:g * 32 + 8 * s2 + 8] = inw / w - (tk8 == t8)
            i15 = np.arange(15)[:, None]
            bands_s[16 * s2:16 * s2 + 15, 128 + g * 32 + 8 * s2:128 + g * 32 + 8 * s2 + 8] = ((t8 + 15 - i15) < w) / w

    pt = np.asarray(page_table, np.int32)
    st = np.asarray(state_pool, f32).reshape(32, 15, 2048)

    in_maps = []
    for c in range(NCORE):
        xall = np.zeros((NTOK, D), f32)
        for j in range(NT):
            tt = c + 8 * j
            xall[j * 128:(j + 1) * 128] = xp[tt * 128:(tt + 1) * 128]
            if tt > 0:
                xall[1024 + 16 * j:1024 + 16 * j + 16] = xp[tt * 128 - 16:tt * 128]
        xall[1152:1184] = xs[4 * c:4 * c + 4].reshape(32, D)
        m = np.arange(GEXT)
        gext = np.ascontiguousarray(gvals(m - 127 + 128 * (c - 7)).T)
        bands = bands_base.copy()
        if c != 0:
            bands[4:8] = bands[0:4]
        bands = np.ascontiguousarray(bands.transpose(1, 0, 2).reshape(128, 40 * 128)).astype(bf)
        bmask = np.zeros((128, 8, 32), f32)
        for j in range(8):
            npast = (c + 8 * j) // 2
            bmask[:, j, npast] = 1e30
            bmask[:, j, npast + 1:] = -1e30
        in_maps.append({
            "xall": xall, "w_in": w_in2, "w_out": w_out2, "w_pool": w_pool2, "gain_in": gain_in,
            "gain_out": gain_out, "pscale": pscale, "cache_k": ck, "cache_v": cv,
            "ptab": np.ascontiguousarray(pt[4 * c:4 * c + 4].reshape(1, 256)),
            "stpool": np.ascontiguousarray(st[4 * c:4 * c + 4].reshape(60, 2048)),
            "gext": gext, "rbs": rbs, "rbn": rbn, "bands": bands, "bands_s": bands_s.astype(bf),
            "bmask": bmask.reshape(128, 256), "ident": ident.astype(bf), "identf": ident,
            "sel": selm.astype(bf), "iota": iota, "jrev": np.ascontiguousarray(ident[::-1]).astype(bf),
        })

    if "nc" not in _NC_CACHE:
        _NC_CACHE["nc"] = build_nc()
    nc = _NC_CACHE["nc"]
    res = run_bass_kernel_spmd(nc, in_maps, core_ids=list(range(NCORE)))
    outs = res.results

    y_prompt = np.zeros((1, SEQ, D), f32)
    k_prompt = np.zeros((1, 1, SEQ, 4, 128), f32)
    v_prompt = np.zeros((1, 1, SEQ, 4, 128), f32)
    y_sample = np.zeros((32, 8, D), f32)
    k_sample = np.zeros((1, 32, 8, 4, 128), f32)
    v_sample = np.zeros((1, 32, 8, 4, 128), f32)
    pool_sample = np.zeros((1, 32, 15, 2048), f32)
    for c in range(NCORE):
        o = outs[c]
        for j in range(NT):
            tt = c + 8 * j
            y_prompt[0, tt * 128:(tt + 1) * 128] = o["y"][j * 128:(j + 1) * 128]
            k_prompt[0, 0, tt * 128:(tt + 1) * 128] = o["kp"][j * 128:(j + 1) * 128].reshape(128, 4, 128)
            v_prompt[0, 0, tt * 128:(tt + 1) * 128] = o["vp"][j * 128:(j + 1) * 128].reshape(128, 4, 128)
        y_sample[4 * c:4 * c + 4] = o["y"][1024:1056].reshape(4, 8, D)
        k_sample[0, 4 * c:4 * c + 4] = o["ks"].reshape(4, 8, 4, 128)
        v_sample[0, 4 * c:4 * c + 4] = o["vs"].reshape(4, 8, 4, 128)
        pool_sample[0, 4 * c:4 * c + 4] = o["ps"].reshape(4, 15, 2048)
    pool_prompt = np.asarray(outs[7]["pp"], f32).reshape(1, 1, 15, 2048)
    return (y_prompt, y_sample, k_prompt, v_prompt, pool_prompt, k_sample, v_sample, pool_sample)
```
